# Optimizing a Trainium2 kernel written in Bass

```python
import math
import jax, jax.numpy as jnp
from jax import lax
import numpy as np

D_MODEL = 1024
BATCH = 8
SEQ = 4096
DEPTH = 2

CHUNK = 64
N_MIXERS = 2
D_FF = 2816
S5_WIDTH = D_MODEL
S5_GROUP_CH = 16
S5_GROUPS = S5_WIDTH // S5_GROUP_CH
S5_STATE = 64
DT_MIN = 0.001
DT_MAX = 0.1
GMLP_BLOCK = 128
GMLP_WIDTH = D_MODEL
GMLP_HEADS = 8
GMLP_HEAD_CH = GMLP_WIDTH // GMLP_HEADS
DN_ALPHA = (2.0 * DEPTH) ** 0.25
DN_BETA = (8.0 * DEPTH) ** -0.25
LN_EPS = 1e-5
N_S5_LAYERS = (DEPTH + 1) // 2
N_GMLP_LAYERS = DEPTH // 2

kernel_name = "hybrid_s5_gmlp_macaron_deepnorm"


def _layernorm(x, g, b):
    xf = x.astype(jnp.float32)
    mu = jnp.mean(xf, axis=-1, keepdims=True)
    var = jnp.mean(jnp.square(xf - mu), axis=-1, keepdims=True)
    y = (xf - mu) * lax.rsqrt(var + LN_EPS)
    return (y * g.astype(jnp.float32) + b.astype(jnp.float32)).astype(x.dtype)


def _swiglu(x, w_in, w_out):
    gate, up = jnp.split(x @ w_in, 2, axis=-1)
    return (jax.nn.silu(gate) * up) @ w_out


def _complex_linear_combine(earlier, later):
    a1r, a1i, b1r, b1i = earlier
    a2r, a2i, b2r, b2i = later
    ar = a2r * a1r - a2i * a1i
    ai = a2r * a1i + a2i * a1r
    br = a2r * b1r - a2i * b1i + b2r
    bi = a2r * b1i + a2i * b1r + b2i
    return (ar, ai, br, bi)


def _s5_mixer(x, w_in, a_re, a_im, log_dt, b_re, b_im, c_re, c_im, d, w_glu, b_glu, w_out):
    f32 = jnp.float32
    bsz, seq, _ = x.shape
    u = (x @ w_in).astype(f32)
    ug = u.reshape(bsz, seq, S5_GROUPS, S5_GROUP_CH)
    ar = a_re.astype(f32)
    ai = a_im.astype(f32)
    dt = jnp.exp(log_dt.astype(f32))[:, None]
    mag = jnp.exp(ar * dt)
    lam_r = mag * jnp.cos(ai * dt)
    lam_i = mag * jnp.sin(ai * dt)
    den = ar * ar + ai * ai
    coef_r = ((lam_r - 1.0) * ar + lam_i * ai) / den
    coef_i = (lam_i * ar - (lam_r - 1.0) * ai) / den
    br = b_re.astype(f32)
    bi = b_im.astype(f32)
    bbar_r = coef_r[..., None] * br - coef_i[..., None] * bi
    bbar_i = coef_r[..., None] * bi + coef_i[..., None] * br
    bu_r = jnp.einsum('blgc,gpc->lbgp', ug, bbar_r)
    bu_i = jnp.einsum('blgc,gpc->lbgp', ug, bbar_i)
    lam_r_seq = jnp.broadcast_to(lam_r[None, None], (seq, 1, S5_GROUPS, S5_STATE))
    lam_i_seq = jnp.broadcast_to(lam_i[None, None], (seq, 1, S5_GROUPS, S5_STATE))
    _, _, h_r, h_i = lax.associative_scan(
        _complex_linear_combine, (lam_r_seq, lam_i_seq, bu_r, bu_i), axis=0)
    y = (jnp.einsum('lbgp,gcp->blgc', h_r, c_re.astype(f32))
         - jnp.einsum('lbgp,gcp->blgc', h_i, c_im.astype(f32)))
    y = y.reshape(bsz, seq, S5_WIDTH) + d.astype(f32) * u
    y = jax.nn.gelu(y)
    z = y * jax.nn.sigmoid(y @ w_glu.astype(f32) + b_glu.astype(f32))
    return z.astype(x.dtype) @ w_out


def _chunk_causal_block_mask():
    pos = jnp.arange(GMLP_BLOCK)
    return (pos[None, :] // CHUNK) <= (pos[:, None] // CHUNK)


def _gmlp_mixer(x, w_in, b_in, ln_g, ln_b, w_s, b_s, w_out):
    bsz, seq, _ = x.shape
    z = jax.nn.gelu(x @ w_in + b_in)
    u, v = jnp.split(z, 2, axis=-1)
    v = _layernorm(v, ln_g, ln_b)
    v = v.reshape(bsz, seq // GMLP_BLOCK, GMLP_BLOCK, GMLP_HEADS, GMLP_HEAD_CH)
    w = jnp.where(_chunk_causal_block_mask()[None], w_s, jnp.zeros((), w_s.dtype))
    s = jnp.einsum('hij,bnjhc->bnihc', w, v) + jnp.transpose(b_s)[:, :, None]
    gated = u * s.reshape(bsz, seq, GMLP_WIDTH)
    return gated @ w_out


def setup_inputs(seed: int = 0) -> dict:
    key = jax.random.key(seed)
    ks = jax.random.split(key, 32)
    f32 = jnp.float32
    nrm = lambda k, shape, scale: jax.random.normal(k, shape, f32) * scale
    x = jax.random.normal(ks[0], (BATCH, SEQ, D_MODEL), f32)
    ln_g = 1.0 + nrm(ks[1], (DEPTH, 3, D_MODEL), 0.02)
    ln_b = nrm(ks[2], (DEPTH, 3, D_MODEL), 0.02)
    ffn_w_in = nrm(ks[3], (DEPTH, 2, D_MODEL, 2 * D_FF), D_MODEL ** -0.5)
    ffn_w_out = nrm(ks[4], (DEPTH, 2, D_FF, D_MODEL), DN_BETA * D_FF ** -0.5)
    na = N_S5_LAYERS
    s5_w_in = nrm(ks[5], (na, D_MODEL, S5_WIDTH), D_MODEL ** -0.5)
    s5_a_re = -0.5 + nrm(ks[6], (na, S5_GROUPS, S5_STATE), 0.01)
    s5_a_im = (math.pi * jnp.arange(S5_STATE, dtype=f32))[None, None] + nrm(ks[7], (na, S5_GROUPS, S5_STATE), 0.01)
    s5_log_dt = jax.random.uniform(ks[8], (na, S5_GROUPS), f32, math.log(DT_MIN), math.log(DT_MAX))
    s5_b_re = nrm(ks[9], (na, S5_GROUPS, S5_STATE, S5_GROUP_CH), (2.0 * S5_GROUP_CH) ** -0.5)
    s5_b_im = nrm(ks[10], (na, S5_GROUPS, S5_STATE, S5_GROUP_CH), (2.0 * S5_GROUP_CH) ** -0.5)
    s5_c_re = nrm(ks[11], (na, S5_GROUPS, S5_GROUP_CH, S5_STATE), (2.0 * S5_STATE) ** -0.5)
    s5_c_im = nrm(ks[12], (na, S5_GROUPS, S5_GROUP_CH, S5_STATE), (2.0 * S5_STATE) ** -0.5)
    s5_d = nrm(ks[13], (na, S5_WIDTH), 1.0)
    s5_w_glu = nrm(ks[14], (na, S5_WIDTH, S5_WIDTH), S5_WIDTH ** -0.5)
    s5_b_glu = nrm(ks[15], (na, S5_WIDTH), 0.02)
    s5_w_out = nrm(ks[16], (na, S5_WIDTH, D_MODEL), DN_BETA * S5_WIDTH ** -0.5)
    nb = N_GMLP_LAYERS
    g_w_in = nrm(ks[17], (nb, D_MODEL, 2 * GMLP_WIDTH), D_MODEL ** -0.5)
    g_b_in = nrm(ks[18], (nb, 2 * GMLP_WIDTH), 0.02)
    g_ln_g = 1.0 + nrm(ks[19], (nb, GMLP_WIDTH), 0.02)
    g_ln_b = nrm(ks[20], (nb, GMLP_WIDTH), 0.02)
    g_w_s = nrm(ks[21], (nb, GMLP_HEADS, GMLP_BLOCK, GMLP_BLOCK), GMLP_BLOCK ** -0.5)
    g_b_s = 1.0 + nrm(ks[22], (nb, GMLP_HEADS, GMLP_BLOCK), 0.1)
    g_w_out = nrm(ks[23], (nb, GMLP_WIDTH, D_MODEL), DN_BETA * GMLP_WIDTH ** -0.5)
    return {
        "x": x, "ln_g": ln_g, "ln_b": ln_b, "ffn_w_in": ffn_w_in, "ffn_w_out": ffn_w_out,
        "s5_w_in": s5_w_in, "s5_a_re": s5_a_re, "s5_a_im": s5_a_im, "s5_log_dt": s5_log_dt,
        "s5_b_re": s5_b_re, "s5_b_im": s5_b_im, "s5_c_re": s5_c_re, "s5_c_im": s5_c_im,
        "s5_d": s5_d, "s5_w_glu": s5_w_glu, "s5_b_glu": s5_b_glu, "s5_w_out": s5_w_out,
        "g_w_in": g_w_in, "g_b_in": g_b_in, "g_ln_g": g_ln_g, "g_ln_b": g_ln_b,
        "g_w_s": g_w_s, "g_b_s": g_b_s, "g_w_out": g_w_out,
    }


def reference(x, ln_g, ln_b, ffn_w_in, ffn_w_out,
              s5_w_in, s5_a_re, s5_a_im, s5_log_dt, s5_b_re, s5_b_im, s5_c_re, s5_c_im,
              s5_d, s5_w_glu, s5_b_glu, s5_w_out,
              g_w_in, g_b_in, g_ln_g, g_ln_b, g_w_s, g_b_s, g_w_out):
    h = x
    for i in range(DEPTH):
        h = _layernorm(DN_ALPHA * h + 0.5 * _swiglu(h, ffn_w_in[i, 0], ffn_w_out[i, 0]), ln_g[i, 0], ln_b[i, 0])
        j = i // N_MIXERS
        if i % N_MIXERS == 0:
            mix = _s5_mixer(h, s5_w_in[j], s5_a_re[j], s5_a_im[j], s5_log_dt[j], s5_b_re[j], s5_b_im[j],
                            s5_c_re[j], s5_c_im[j], s5_d[j], s5_w_glu[j], s5_b_glu[j], s5_w_out[j])
        else:
            mix = _gmlp_mixer(h, g_w_in[j], g_b_in[j], g_ln_g[j], g_ln_b[j], g_w_s[j], g_b_s[j], g_w_out[j])
        h = _layernorm(DN_ALPHA * h + mix, ln_g[i, 1], ln_b[i, 1])
        h = _layernorm(DN_ALPHA * h + 0.5 * _swiglu(h, ffn_w_in[i, 1], ffn_w_out[i, 1]), ln_g[i, 2], ln_b[i, 2])
    return h
```

```python
import math
from contextlib import ExitStack

import numpy as np
import concourse.bass as bass
import concourse.mybir as mybir
from concourse.bass_utils import run_bass_kernel_spmd

F32 = mybir.dt.float32
BF16 = mybir.dt.bfloat16
I32 = mybir.dt.int32
AF = mybir.ActivationFunctionType
ALU = mybir.AluOpType
ENGS = ["tensor", "vector", "scalar", "gpsimd", "sync"]

D = 1024
SEQ = 4096
DFF = 2816
TT = 512
NT = SEQ // TT
KT = D // 128
FT = DFF // 128
ALPHA = 4.0 ** 0.25
C_FFN = 0.5 / ALPHA
C_MIX = 1.0 / ALPHA
EPS_MAIN = 1e-5 / (ALPHA * ALPHA)
EPS_G = 1e-5
NSLOT = 5
SLOTW = 4096
NCH = TT // 8
KLIST = [0, -1, -2, -3, -4, -5, -6, -7, 0, 1, 2, 3, 4, 5, 6, 7, 8, 9, 10, 11, 12, 13, 14, 15]
NK = len(KLIST)


class Res:
    __slots__ = ("name", "w", "r")

    def __init__(self, name):
        self.name = name
        self.w = []
        self.r = []


class Ticket:
    __slots__ = ("key", "val", "known", "eng")

    def __init__(self, key, val, known, eng):
        self.key, self.val, self.known, self.eng = key, val, known, eng


class Prog:
    def __init__(self, nc):
        self.nc = nc
        self.streams = {e: [] for e in ENGS}
        self.seen = {e: {} for e in ENGS}
        self.count = {}
        self.sems = {}
        self.last = {}
        self.arena_keys = set()

    def sem(self, key):
        if key not in self.sems:
            self.sems[key] = self.nc.alloc_semaphore(name="s_" + key)
            self.count[key] = 0
        return self.sems[key]

    def _deps(self, eng, reads, writes, is_dma):
        deps = []
        for r in reads:
            deps.extend(r.w)
        same_ok = (eng == "tensor") and not is_dma
        for w in writes:
            for t in w.w:
                if not (same_ok and t.key == eng):
                    deps.append(t)
            for t in w.r:
                if not (same_ok and t.key == eng):
                    deps.append(t)
        return deps

    @staticmethod
    def _record(t, reads, writes):
        for w in writes:
            if w.r:
                w.w = [t]
                w.r = []
            else:
                w.w = [x for x in w.w if x.key != t.key] + [t]
        for r in reads:
            if r not in writes:
                r.r.append(t)

    def _emit_waits(self, eng, deps):
        seen = self.seen[eng]
        need = {}
        for t in deps:
            if seen.get(t.key, 0) < t.val:
                need[t.key] = max(need.get(t.key, 0), t.val)
        implied = {}
        for t in deps:
            for k, v in t.known.items():
                if implied.get(k, 0) < v:
                    implied[k] = v
        for k, v in need.items():
            if implied.get(k, 0) >= v:
                continue
            sem = self.sems[k]
            self.streams[eng].append(lambda e, sem=sem, v=v: e.wait_ge(sem, v))
        for k, v in need.items():
            if seen.get(k, 0) < v:
                seen[k] = v
        for k, v in implied.items():
            if seen.get(k, 0) < v:
                seen[k] = v

    def task(self, eng, instrs, reads=(), writes=()):
        if not isinstance(instrs, (list, tuple)):
            instrs = [instrs]
        deps = self._deps(eng, reads, writes, False)
        self._emit_waits(eng, deps)
        sem = self.sem(eng)
        self.count[eng] += 1
        val = self.count[eng]
        known = dict(self.seen[eng])
        n = len(instrs)
        st = self.streams[eng]
        for i, f in enumerate(instrs):
            if i == n - 1:
                st.append(lambda e, f=f, sem=sem: f(e).then_inc(sem, 1))
            else:
                st.append(f)
        t = Ticket(eng, val, known, eng)
        self.last[eng] = t
        self._record(t, reads, writes)
        return t

    def dma(self, eng, pairs, reads=(), writes=(), key=None, arena=False, nc_ok=False):
        deps = self._deps(eng, reads, writes, True)
        self._emit_waits(eng, deps)
        if key is None:
            key = ("L_" + writes[0].name) if writes else ("S_" + reads[0].name)
        sem = self.sem(key)
        known = dict(self.seen[eng])
        st = self.streams[eng]
        for (o, i) in pairs:
            self.count[key] += 16
            if nc_ok:
                st.append(lambda e, o=o, i=i, sem=sem: e.dma_start(out=o, in_=i, allow_slow_non_contiguous=True).then_inc(sem, 16))
            else:
                st.append(lambda e, o=o, i=i, sem=sem: e.dma_start(out=o, in_=i).then_inc(sem, 16))
        t = Ticket(key, self.count[key], known, eng)
        self.last[key] = t
        if arena:
            self.arena_keys.add(key)
        self._record(t, reads, writes)
        return t

    def wait_all(self, eng, tickets):
        self._emit_waits(eng, list(tickets))

    def barrier(self, sync=False):
        cengs = ["tensor", "vector", "scalar", "gpsimd"]
        ts = [self.last[k] for k in cengs if k in self.last]
        ts += [self.last[k] for k in self.arena_keys if k in self.last]
        for e in cengs + (["sync"] if sync else []):
            self._emit_waits(e, [t for t in ts if t.key != e])

    def emit(self):
        nc = self.nc
        with nc.Block() as block:
            for ename in ENGS:
                ops = self.streams[ename]

                def body(e, ops=ops):
                    for f in ops:
                        f(e)

                getattr(block, ename)(body)


class Arena:
    def __init__(self, t, nwords):
        self.t, self.n, self.off = t, nwords, 0

    def reset(self):
        self.off = 0

    def alloc(self, shape, dtype, parts=128):
        n = 1
        for s in shape:
            n *= s
        words = n if dtype in (F32, I32) else (n + 1) // 2
        assert self.off + words <= self.n, ("arena overflow", self.off, words, self.n)
        ap = self.t[0:parts, self.off:self.off + words]
        self.off += words
        if dtype != F32:
            ap = ap.bitcast(dtype)
        if len(shape) == 1:
            pass
        elif len(shape) == 2:
            ap = ap.rearrange("p (a b) -> p a b", b=shape[1])
        elif len(shape) == 3:
            ap = ap.rearrange("p (a b c) -> p a b c", b=shape[1], c=shape[2])
        elif len(shape) == 4:
            ap = ap.rearrange("p (a b c d) -> p a b c d", b=shape[1], c=shape[2], d=shape[3])
        return ap


def build(stages=None, nt=NT, debug=None, seq=SEQ, sw=()):
    if stages is None:
        stages = ["f00", "mix0", "f01", "f10", "mix1", "f11"]
    nc = bass.Bass("TRN2", target_bir_lowering=False)

    in_names = []

    def din(name, shape, dt=F32):
        in_names.append(name)
        return nc.dram_tensor(name, list(shape), dt, kind="ExternalInput").ap()

    def dint(name, shape, dt=BF16):
        return nc.dram_tensor(name, list(shape), dt, kind="Internal").ap()

    x_d = din("x_tok", [seq, D])
    out_d = nc.dram_tensor("y_tok", [seq, D], F32, kind="ExternalOutput").ap()
    has_f = any(s_[0] == "f" for s_ in stages)
    has_s5 = "mix0" in stages
    has_g = "mix1" in stages
    if has_f:
        ffn_w_in = din("ffn_w_in", [2, 2, D, 2 * DFF])
        ffn_w_out = din("ffn_w_out", [2, 2, DFF, D])
    if has_s5:
        s5_w_in = din("s5_w_in", [D, D])
        s5_w_glu = din("s5_w_glu", [D, D])
        s5_w_out = din("s5_w_out", [D, D])
    if has_g:
        g_w_in = din("g_w_in", [D, 2 * D])
        g_w_out = din("g_w_out", [D, D])
        gws_d = din("gws", [128, 8, 128])
    smallc_d = din("smallc", [128, 136])
    gbs_d = din("gbs", [128, 8, 128])
    ident_d = din("ident", [128, 128])
    if has_s5:
        s5are_d = din("s5are", [64, 64])
        s5aim_d = din("s5aim", [64, 64])
        s5ldt_d = din("s5ldt", [64, 64])
        s5bre_d = din("s5bre", [64, 64 * 16])
        s5bim_d = din("s5bim", [64, 64 * 16])
        s5cre_d = din("s5cre", [64, 64 * 16])
        s5cim_d = din("s5cim", [64, 64 * 16])
        s5dcol_d = din("s5dcol", [128, 64])
        kvec_d = din("kvec", [64, NK])
        cmask_d = din("cmask", [128, 128])
        selin_d = din("selin", [128, 64 * 128])
        selout_d = din("selout", [128, 64 * 128])

    if has_f:
        c_ffn_in = dint("c_ffn_in", [2, 2, D, 2 * DFF])
        c_ffn_out = dint("c_ffn_out", [2, 2, DFF, D])
    if has_s5:
        c_s5_in = dint("c_s5_in", [D, D])
        c_s5_glu = dint("c_s5_glu", [D, D])
        c_s5_out = dint("c_s5_out", [D, D])
        c_selin = dint("c_selin", [2, 128, SLOTW])
        c_selout = dint("c_selout", [2, 128, SLOTW])
        c_bw = dint("c_bw", [2, 128, SLOTW])
        c_m = dint("c_m", [2, 128, SLOTW])
        c_q2 = dint("c_q2", [4, 64, SLOTW])
    if has_g:
        c_g_in = dint("c_g_in", [D, 2 * D])
        c_g_out = dint("c_g_out", [D, D])

    dbg_d = None
    if debug:
        dbg_d = {k: nc.dram_tensor("dbg_" + k, list(shp), F32, kind="ExternalOutput").ap()
                 for k, shp in debug.items()}

    es = ExitStack()
    with es:
        es.enter_context(nc.cleanup_on_exit())
        sb = lambda name, shape, dt=F32: es.enter_context(nc.sbuf_tensor(name, list(shape), dt))
        h32 = sb("h32", [128, KT, TT])
        ident32 = sb("ident32", [128, 128])
        ARW = 24 * 1024
        arena_t = sb("arena", [128, ARW])
        ar = Arena(arena_t, ARW)
        x16 = sb("x16", [128, KT, TT], BF16)
        slots = [sb("slot%d" % i, [128, SLOTW], BF16) for i in range(NSLOT)]
        ones16 = sb("ones16", [128, 128], BF16)
        ident16 = sb("ident16", [128, 128], BF16)
        smallc = sb("smallc_s", [128, 136])
        lng = smallc[:, 0:48]
        lnb = smallc[:, 48:96]
        bglu = smallc[:, 96:104]
        gbin = smallc[:, 104:120]
        glng = smallc[:, 120:128]
        glnb = smallc[:, 128:136]
        gbs = sb("gbs_s", [128, 8, 128])
        wmT16 = sb("wmT16", [128, 8, 128], BF16)
        cvec = sb("cvec", [128, 8])
        s5CA = sb("s5CA", [64, 2, 64])
        s5CB = sb("s5CB", [64, 2, 64])
        hseq_keep = None
        psb = [es.enter_context(nc.psum_tensor("ps%d" % i, [128, 512], F32)) for i in range(8)]

        P = Prog(nc)
        R = {}

        def res(name):
            if name not in R:
                R[name] = Res(name)
            return R[name]

        r_ps = [res("ps%d" % i) for i in range(8)]
        r_slot = [res("slot%d" % i) for i in range(NSLOT)]
        ps_ctr = [0]

        def next_ps():
            i = ps_ctr[0] % 8
            ps_ctr[0] += 1
            return psb[i], r_ps[i]

        def conv(dst, src, rname, rows, cols):
            r = res(rname)
            nblk = max(1, rows // 256)
            pairs = []
            for i in range(nblk):
                rs = slice(i * rows // nblk, (i + 1) * rows // nblk)
                if cols > 2048:
                    b = 1408
                    pairs.append((dst[rs, :].rearrange("r (a b) -> r a b", b=b),
                                  src[rs, :].rearrange("r (a b) -> r a b", b=b)))
                else:
                    pairs.append((dst[rs, :], src[rs, :]))
            for pr in pairs:
                P.dma("gpsimd", [pr], writes=[r], key="cv_" + rname)
            return r

        need_l0 = any(s in stages for s in ("f00", "mix0", "f01"))
        need_l1 = any(s in stages for s in ("f10", "mix1", "f11"))
        order = []
        for l, f, st in ((0, 0, "f00"), (0, 1, "f01"), (1, 0, "f10"), (1, 1, "f11")):
            if st in stages:
                order.append(("ffn", l, f))
        def conv_ffn(l, f):
            conv(c_ffn_in[l, f], ffn_w_in[l, f], "cfi%d%d" % (l, f), D, 2 * DFF)
            conv(c_ffn_out[l, f], ffn_w_out[l, f], "cfo%d%d" % (l, f), DFF, D)

        if "f00" in stages:
            conv_ffn(0, 0)
        if "mix0" in stages:
            conv(c_s5_in, s5_w_in, "cs5i", D, D)
            conv(c_s5_glu, s5_w_glu, "cs5g", D, D)
            conv(c_s5_out, s5_w_out, "cs5o", D, D)
        if "f01" in stages:
            conv_ffn(0, 1)
        if "f10" in stages:
            conv_ffn(1, 0)
        if "mix1" in stages:
            conv(c_g_in, g_w_in, "cgi", D, 2 * D)
            conv(c_g_out, g_w_out, "cgo", D, D)
        if "f11" in stages:
            conv_ffn(1, 1)

        r_const = res("const")
        P.dma("sync", [(smallc[:], smallc_d), (gbs[:], gbs_d), (ident32[:], ident_d)], writes=[r_const])
        r_c2 = res("const2")
        P.task("vector", [lambda e: e.memset(ones16[:], 1.0),
                          lambda e: e.memset(cvec[:, 0:1], EPS_MAIN),
                          lambda e: e.memset(cvec[:, 1:2], EPS_G),
                          lambda e: e.memset(cvec[:, 2:3], -math.pi),
                          lambda e: e.memset(cvec[:, 3:4], 0.0),
                          lambda e: e.tensor_copy(out=ident16[:], in_=ident32[:])],
               reads=[r_const], writes=[r_c2])

        wq = []
        wq_issued = [0]
        wq_used = [0]
        wq_done = [0]

        def w_enqueue(mk_pairs, rres):
            wq.append((mk_pairs, rres))

        def w_issue():
            lim = min(wq_done[0] + NSLOT, len(wq))
            while wq_issued[0] < lim:
                i = wq_issued[0]
                mk_pairs, rres = wq[i]
                s_ = i % NSLOT
                P.dma("sync", mk_pairs(slots[s_]), reads=[rres], writes=[r_slot[s_]], key="L_slot%d" % s_)
                wq_issued[0] += 1

        def w_next():
            i = wq_used[0]
            assert i < wq_done[0] + NSLOT, "slot pool exhausted"
            w_issue()
            wq_used[0] += 1
            s_ = i % NSLOT
            return slots[s_], r_slot[s_]

        def w_done(k=1):
            wq_done[0] += k
            w_issue()

        def proj_blocks(cache, ncols):
            v = cache.rearrange("(kt p) n -> p kt n", p=128)
            rr = cache_res[id_of(cache)]
            for b in range(ncols // 512):
                def mk(slot, b=b):
                    return [(slot[:, :].rearrange("p (kt n) -> p kt n", n=512), v[:, :, b * 512:(b + 1) * 512])]
                w_enqueue(mk, rr)

        cache_res = {}

        def id_of(ap):
            return ap.tensor.name

        if has_s5:
            for (ap_, nm) in ((c_s5_in, "cs5i"), (c_s5_glu, "cs5g"), (c_s5_out, "cs5o")):
                cache_res[id_of(ap_)] = res(nm)
        if has_g:
            for (ap_, nm) in ((c_g_in, "cgi"), (c_g_out, "cgo")):
                cache_res[id_of(ap_)] = res(nm)

        def proj(src16, r_src, nblocks, evac):
            for b in range(nblocks):
                slot, r_sl = w_next()
                wv = slot[:, :].rearrange("p (kt n) -> p kt n", n=512)
                for o in range(4):
                    bank, r_b = next_ps()
                    ins = []
                    for kt in range(KT):
                        ins.append(lambda e, bank=bank, wv=wv, kt=kt, o=o: e.matmul(
                            bank[:, :], wv[:, kt, o * 128:(o + 1) * 128], src16[:, kt, :],
                            start=(kt == 0), stop=(kt == KT - 1)))
                    P.task("tensor", ins, reads=[r_sl, r_src], writes=[r_b])
                    evac(b * 4 + o, bank, r_b)
                w_done()

        def gelu_tanh(src_ap, r_src, out32, r_out32, out16, r_out16, bias_ap, tmp, r_tmp, n=TT):
            xb, t2 = tmp
            r_xb, r_t2 = r_tmp
            if bias_ap is not None:
                P.task("scalar", lambda e: e.activation(out=xb, in_=src_ap, func=AF.Identity, bias=bias_ap, scale=1.0),
                       reads=[r_src, r_c2], writes=[r_xb])
            else:
                P.task("scalar", lambda e: e.copy(out=xb, in_=src_ap), reads=[r_src], writes=[r_xb])
            P.task("scalar", lambda e: e.activation(out=t2, in_=xb, func=AF.Square), reads=[r_xb], writes=[r_t2])
            P.task("vector", lambda e: e.tensor_scalar(out=t2, in0=t2, scalar1=0.044715, scalar2=1.0,
                                                       op0=ALU.mult, op1=ALU.add), reads=[r_t2], writes=[r_t2])
            P.task("vector", lambda e: e.tensor_tensor(out=t2, in0=t2, in1=xb, op=ALU.mult),
                   reads=[r_t2, r_xb], writes=[r_t2])
            P.task("scalar", lambda e: e.activation(out=t2, in_=t2, func=AF.Sigmoid, scale=2.0 * math.sqrt(2.0 / math.pi)),
                   reads=[r_t2], writes=[r_t2])
            if out32 is not None:
                P.task("vector", lambda e: e.tensor_tensor(out=out32, in0=t2, in1=xb, op=ALU.mult),
                       reads=[r_t2, r_xb], writes=[r_out32])
            if out16 is not None:
                P.task("gpsimd", lambda e: e.tensor_tensor(out=out16, in0=t2, in1=xb, op=ALU.mult),
                       reads=[r_t2, r_xb], writes=[r_out16])

        def layer_norm(src32, r_src, g_ap, b_ap, eps_col, out32, r_o32, out16, r_o16, sc):
            t16, tsq16 = sc["t16"], sc["tsq16"]
            r_t16, r_tsq = sc["r_t16"], sc["r_tsq16"]
            P.task("scalar", lambda e: e.copy(out=t16[:, :, :], in_=src32[:, :, :]), reads=[r_src], writes=[r_t16])
            P.task("scalar", lambda e: e.activation(out=tsq16[:, :, :], in_=src32[:, :, :], func=AF.Square),
                   reads=[r_src], writes=[r_tsq])
            b1, r_b1 = next_ps()
            b2, r_b2 = next_ps()
            P.task("tensor", [lambda e, kt=kt: e.matmul(b1[:, :], ones16[:, :], t16[:, kt, :], start=(kt == 0), stop=(kt == KT - 1))
                              for kt in range(KT)], reads=[r_t16, r_c2], writes=[r_b1])
            P.task("tensor", [lambda e, kt=kt: e.matmul(b2[:, :], ones16[:, :], tsq16[:, kt, :], start=(kt == 0), stop=(kt == KT - 1))
                              for kt in range(KT)], reads=[r_tsq, r_c2], writes=[r_b2])
            mean, rstd, nmr, tmpa = sc["mean"], sc["rstd"], sc["nmr"], sc["tmpa"]
            r_mean, r_rstd, r_nmr, r_tmpa = res("ln_mean"), res("ln_rstd"), res("ln_nmr"), res("ln_tmpa")
            P.task("scalar", lambda e: e.mul(out=mean, in_=b1[:, :], mul=1.0 / D), reads=[r_b1], writes=[r_mean])
            P.task("vector", lambda e: e.tensor_tensor(out=tmpa, in0=mean, in1=mean, op=ALU.mult), reads=[r_mean], writes=[r_tmpa])
            P.task("vector", lambda e: e.scalar_tensor_tensor(out=tmpa, in0=b2[:, :], scalar=1.0 / D, in1=tmpa,
                                                              op0=ALU.mult, op1=ALU.subtract),
                   reads=[r_b2, r_tmpa], writes=[r_tmpa])
            P.task("scalar", lambda e: e.activation(out=tmpa, in_=tmpa, func=AF.Sqrt, bias=cvec[:, eps_col:eps_col + 1], scale=1.0),
                   reads=[r_tmpa, r_c2], writes=[r_tmpa])
            P.task("vector", lambda e: e.reciprocal(out=rstd, in_=tmpa), reads=[r_tmpa], writes=[r_rstd])
            P.task("vector", lambda e: e.scalar_tensor_tensor(out=nmr, in0=mean, scalar=-1.0, in1=rstd, op0=ALU.mult, op1=ALU.mult),
                   reads=[r_mean, r_rstd], writes=[r_nmr])
            ys = [sc["ya"], sc["yb"]]
            r_ys = [res("ln_ya"), res("ln_yb")]
            for kt in range(KT):
                y, r_y = ys[kt % 2], r_ys[kt % 2]
                P.task("vector", lambda e, kt=kt, y=y: e.tensor_tensor(out=y, in0=src32[:, kt, :], in1=rstd, op=ALU.mult),
                       reads=[r_src, r_rstd], writes=[r_y])
                P.task("gpsimd", lambda e, y=y: e.tensor_tensor(out=y, in0=y, in1=nmr, op=ALU.add),
                       reads=[r_y, r_nmr], writes=[r_y])
                if out16 is not None:
                    P.task("scalar", lambda e, kt=kt, y=y: e.activation(out=out16[:, kt, :], in_=y, func=AF.Identity,
                                                                        scale=g_ap[:, kt:kt + 1], bias=b_ap[:, kt:kt + 1]),
                           reads=[r_y, r_const], writes=[r_o16])
                if out32 is not None:
                    P.task("gpsimd", lambda e, kt=kt, y=y: e.tensor_scalar(out=out32[:, kt, :], in0=y, scalar1=g_ap[:, kt:kt + 1],
                                                                          scalar2=b_ap[:, kt:kt + 1], op0=ALU.mult, op1=ALU.add),
                           reads=[r_y, r_const], writes=[r_o32])

        r_h32, r_x16 = res("h32"), res("x16")

        def ln_scratch(t16=None, tsq16=None, r_t16=None, r_tsq16=None):
            return dict(t16=t16 if t16 is not None else ar.alloc([KT, TT], BF16),
                        tsq16=tsq16 if tsq16 is not None else ar.alloc([KT, TT], BF16),
                        r_t16=r_t16 or res("ln_t16"), r_tsq16=r_tsq16 or res("ln_tsq16"),
                        mean=ar.alloc([TT], F32), rstd=ar.alloc([TT], F32), nmr=ar.alloc([TT], F32),
                        tmpa=ar.alloc([TT], F32), ya=ar.alloc([TT], F32), yb=ar.alloc([TT], F32))

        def residual_evac(c):
            def ev(ot, bank, r_b):
                P.task("vector", lambda e, ot=ot, bank=bank: e.scalar_tensor_tensor(
                    out=h32[:, ot, :], in0=bank[:, :], scalar=c, in1=h32[:, ot, :], op0=ALU.mult, op1=ALU.add),
                    reads=[r_b, r_h32], writes=[r_h32])
            return ev

        def ffn_enqueue(l, f):
            vin = c_ffn_in[l, f].rearrange("(kt p) n -> p kt n", p=128)
            vout = c_ffn_out[l, f].rearrange("(j p) n -> p j n", p=128)
            rin, rout = res("cfi%d%d" % (l, f)), res("cfo%d%d" % (l, f))
            for jb in range(FT // 2):
                def mk(slot, jb=jb):
                    sv = slot[:, :].rearrange("p (g kt n) -> p g kt n", g=2, n=256)
                    return [(sv[:, 0], vin[:, :, jb * 256:(jb + 1) * 256]),
                            (sv[:, 1], vin[:, :, DFF + jb * 256:DFF + (jb + 1) * 256])]
                w_enqueue(mk, rin)
            for dh in range(2):
                for (j0, nf) in ((0, 8), (8, 8), (16, 6)):
                    def mk(slot, dh=dh, j0=j0, nf=nf):
                        sv = slot[:, 0:nf * 512].rearrange("p (j n) -> p j n", n=512)
                        return [(sv, vout[:, j0:j0 + nf, dh * 512:(dh + 1) * 512])]
                    w_enqueue(mk, rout)

        def ffn(l, f, lnidx):
            P.barrier()
            ar.reset()
            hid = ar.alloc([FT, TT], BF16)
            sgs = [ar.alloc([TT], F32), ar.alloc([TT], F32)]
            sc = ln_scratch()
            r_hid = res("hid")
            r_sg = [res("sg0"), res("sg1")]
            for jb in range(FT // 2):
                slot, r_sl = w_next()
                sv = slot[:, :].rearrange("p (g kt n) -> p g kt n", g=2, n=256)
                for jj in range(2):
                    j = jb * 2 + jj
                    bg, r_bg = next_ps()
                    bu, r_bu = next_ps()
                    for (bank, r_b, gi) in ((bg, r_bg, 0), (bu, r_bu, 1)):
                        P.task("tensor", [lambda e, bank=bank, gi=gi, kt=kt, jj=jj, sv=sv: e.matmul(
                            bank[:, :], sv[:, gi, kt, jj * 128:(jj + 1) * 128], x16[:, kt, :],
                            start=(kt == 0), stop=(kt == KT - 1)) for kt in range(KT)],
                            reads=[r_sl, r_x16], writes=[r_b])
                    sg, r_s = sgs[j % 2], r_sg[j % 2]
                    P.task("scalar", lambda e, sg=sg, bg=bg: e.activation(out=sg, in_=bg[:, :], func=AF.Silu),
                           reads=[r_bg], writes=[r_s])
                    P.task("vector", lambda e, sg=sg, bu=bu, j=j: e.tensor_tensor(out=hid[:, j, :], in0=sg, in1=bu[:, :], op=ALU.mult),
                           reads=[r_s, r_bu], writes=[r_hid])
                w_done()
            ev = residual_evac(C_FFN)
            for dh in range(2):
                banks = [next_ps() for _ in range(4)]
                for (j0, nf) in ((0, 8), (8, 8), (16, 6)):
                    slot, r_sl = w_next()
                    sv = slot[:, 0:nf * 512].rearrange("p (j n) -> p j n", n=512)
                    ins = []
                    for jl in range(nf):
                        j = j0 + jl
                        for dd in range(4):
                            ins.append(lambda e, dd=dd, jl=jl, j=j, sv=sv, bank=banks[dd][0]: e.matmul(
                                bank[:, :], sv[:, jl, dd * 128:(dd + 1) * 128], hid[:, j, :],
                                start=(j == 0), stop=(j == FT - 1)))
                    P.task("tensor", ins, reads=[r_sl, r_hid], writes=[b[1] for b in banks])
                    w_done()
                for dd in range(4):
                    ev(dh * 4 + dd, banks[dd][0], banks[dd][1])
            layer_norm(h32, r_h32, lng[:, lnidx * 8:(lnidx + 1) * 8], lnb[:, lnidx * 8:(lnidx + 1) * 8], 0,
                       h32, r_h32, x16, r_x16, sc)

        def gmlp_setup():
            P.barrier(sync=True)
            ar.reset()
            ws32 = ar.alloc([8, 128], F32)
            r_ws = res("ws32")
            P.dma("sync", [(ws32, gws_d)], writes=[r_ws], key="L_ws32", arena=True)
            r_wm = res("wmT16")
            for hh in range(8):
                bank, r_b = next_ps()
                P.task("tensor", lambda e, hh=hh, bank=bank: e.transpose(out=bank[:, 0:128], in_=ws32[:, hh, :], identity=ident32[:, :]),
                       reads=[r_ws, r_const], writes=[r_b])
                P.task("vector", lambda e, hh=hh, bank=bank: e.tensor_copy(out=wmT16[:, hh, :], in_=bank[:, 0:128]),
                       reads=[r_b], writes=[r_wm])
            P.task("vector", lambda e: e.memset(wmT16[64:128, :, 0:64], 0.0), reads=[], writes=[r_wm])

        def gmlp_enqueue():
            proj_blocks(c_g_in, 2 * D)
            proj_blocks(c_g_out, D)

        def gmlp(lnidx):
            P.barrier()
            ar.reset()
            u32 = ar.alloc([KT, TT], F32)
            v32 = ar.alloc([KT, TT], F32)
            vn16 = ar.alloc([KT, TT], BF16)
            vT16 = ar.alloc([4, 8 * 128], BF16)
            gated16 = ar.alloc([KT, TT], BF16)
            tmps = [(ar.alloc([TT], F32), ar.alloc([TT], F32)) for _ in range(2)]
            r_tmps = [(res("g_xb%d" % i), res("g_t2%d" % i)) for i in range(2)]
            sc = ln_scratch()
            r_u32, r_v32, r_vn16, r_vT, r_gated = res("u32"), res("v32"), res("vn16"), res("vT16"), res("gated16")
            cnt = [0]

            def ev_in(ot, bank, r_b):
                i = cnt[0] % 2
                cnt[0] += 1
                if ot < 8:
                    gelu_tanh(bank[:, :], r_b, u32[:, ot, :], r_u32, None, None, gbin[:, ot:ot + 1], tmps[i], r_tmps[i])
                else:
                    gelu_tanh(bank[:, :], r_b, v32[:, ot - 8, :], r_v32, None, None, gbin[:, ot:ot + 1], tmps[i], r_tmps[i])
            proj(x16, r_x16, 4, ev_in)
            layer_norm(v32, r_v32, glng, glnb, 1, None, None, vn16, r_vn16, sc)
            for blk in range(4):
                for hf in range(2):
                    bank, r_b = next_ps()
                    P.task("tensor", [lambda e, hq=hq, hf=hf, blk=blk, bank=bank: e.matmul(
                        bank[:, hq * 128:(hq + 1) * 128], vn16[:, hf * 4 + hq, blk * 128:(blk + 1) * 128], ident16[:, :], start=True, stop=True)
                        for hq in range(4)], reads=[r_vn16, r_c2], writes=[r_b])
                    P.task("scalar", lambda e, blk=blk, hf=hf, bank=bank: e.copy(out=vT16[:, blk, hf * 512:(hf + 1) * 512], in_=bank[:, :]),
                           reads=[r_b], writes=[r_vT])
            for hh in range(8):
                bank, r_b = next_ps()
                P.task("tensor", [lambda e, hh=hh, blk=blk, bank=bank: e.matmul(
                    bank[:, blk * 128:(blk + 1) * 128], vT16[:, blk, hh * 128:(hh + 1) * 128], wmT16[:, hh, :], start=True, stop=True)
                    for blk in range(4)], reads=[r_vT, res("wmT16")], writes=[r_b])
                tm, r_tm = tmps[hh % 2][0], r_tmps[hh % 2][0]
                P.task("vector", lambda e, hh=hh, bank=bank, tm=tm: e.tensor_tensor(
                    out=tm.rearrange("p (b i) -> p b i", i=128), in0=bank[:, :].rearrange("p (b i) -> p b i", i=128),
                    in1=gbs[:, hh:hh + 1, :].broadcast_to([128, 4, 128]), op=ALU.add),
                    reads=[r_b, r_const], writes=[r_tm])
                P.task("gpsimd", lambda e, hh=hh, tm=tm: e.tensor_tensor(out=gated16[:, hh, :], in0=tm, in1=u32[:, hh, :], op=ALU.mult),
                       reads=[r_tm, r_u32], writes=[r_gated])
            proj(gated16, r_gated, 2, residual_evac(C_MIX))
            layer_norm(h32, r_h32, lng[:, lnidx * 8:(lnidx + 1) * 8], lnb[:, lnidx * 8:(lnidx + 1) * 8], 0,
                       h32, r_h32, x16, r_x16, sc)

        r_ca = res("s5CA")
        r_hs = res("hseq")

        def s5_setup():
            P.barrier(sync=True)
            ar.reset()
            G = 64
            A = lambda shape, dt=F32: ar.alloc(shape, dt, parts=64)
            GC = 4
            Pr, Pi, nPi = A([GC, 8, 16]), A([GC, 8, 16]), A([GC, 8, 16])
            Qr, Qi = A([GC, 8, 16]), A([GC, 8, 16])
            are, aim, ldt = A([G, 1]), A([G, 1]), A([G, 1])
            kv = A([1, NK])
            r_p = res("s5p")
            P.dma("sync", [(are[:, :, 0], s5are_d), (aim[:, :, 0], s5aim_d), (ldt[:, :, 0], s5ldt_d), (kv[:, 0, :], kvec_d)],
                  writes=[r_p], key="L_s5p", arena=True)
            Bre, Bim, Cre, Cim = A([G, 16]), A([G, 16]), A([G, 16]), A([G, 16])
            r_bc = res("s5bc")
            P.dma("sync", [(Bre, s5bre_d.rearrange("p (g c) -> p g c", c=16)), (Bim, s5bim_d.rearrange("p (g c) -> p g c", c=16)),
                           (Cre, s5cre_d.rearrange("p (g c) -> p g c", c=16)), (Cim, s5cim_d.rearrange("p (g c) -> p g c", c=16))],
                  writes=[r_bc], key="L_s5bc", arena=True)
            dcol = ar.alloc([1, 64], F32)[:, 0, :]
            cmask = ar.alloc([1, 128], F32)[:, 0, :]
            r_dm = res("s5dm")
            P.dma("sync", [(dcol, s5dcol_d), (cmask, cmask_d)], writes=[r_dm], key="L_s5dm", arena=True)
            for (srcd, dstc, nm) in ((selin_d, c_selin, "selin"), (selout_d, c_selout, "selout")):
                for hlf in range(2):
                    P.dma("gpsimd", [(dstc[hlf], srcd[:, hlf * SLOTW:(hlf + 1) * SLOTW])], writes=[res("c_" + nm)], key="cv_" + nm)
            seq = []
            r_s = res("s5chain")

            def V(f, extra_r=()):
                P.task("vector", f, reads=[r_s, r_p, r_bc, r_dm] + list(extra_r), writes=[r_s])

            def S(f):
                P.task("scalar", f, reads=[r_s, r_p, r_c2], writes=[r_s])

            dt_ = A([G, 1])
            ardt, aidt = A([G, 1]), A([G, 1])
            S(lambda e: e.activation(out=dt_, in_=ldt, func=AF.Exp))
            V(lambda e: e.tensor_tensor(out=ardt, in0=are, in1=dt_, op=ALU.mult))
            V(lambda e: e.tensor_tensor(out=aidt, in0=aim, in1=dt_, op=ALU.mult))
            E, TH = A([G, NK]), A([G, NK])
            kvb = kv.broadcast_to([64, G, NK])
            V(lambda e: e.tensor_tensor(out=E, in0=ardt.broadcast_to([64, G, NK]), in1=kvb, op=ALU.mult))
            V(lambda e: e.tensor_tensor(out=TH, in0=aidt.broadcast_to([64, G, NK]), in1=kvb, op=ALU.mult))
            MAG = A([G, NK])
            S(lambda e: e.activation(out=MAG, in_=E, func=AF.Exp))
            LR, LI = A([G, NK]), A([G, NK])
            VV, NI, NF = E, A([G, NK], I32), A([G, NK])
            for (dst, off) in ((LI, 64.5), (LR, 64.75)):
                V(lambda e, off=off: e.tensor_scalar(out=VV, in0=TH, scalar1=1.0 / (2 * math.pi), scalar2=off, op0=ALU.mult, op1=ALU.add))
                V(lambda e: e.tensor_copy(out=NI, in_=VV))
                V(lambda e: e.tensor_copy(out=NF, in_=NI))
                V(lambda e: e.tensor_tensor(out=VV, in0=VV, in1=NF, op=ALU.subtract))
                V(lambda e: e.tensor_scalar(out=NF, in0=VV, scalar1=0.0, scalar2=None, op0=ALU.is_lt))
                V(lambda e: e.tensor_tensor(out=VV, in0=VV, in1=NF, op=ALU.add))
                S(lambda e, dst=dst: e.activation(out=dst, in_=VV, func=AF.Sin, scale=2 * math.pi, bias=cvec[0:64, 2:3]))
                V(lambda e, dst=dst: e.tensor_tensor(out=dst, in0=dst, in1=MAG, op=ALU.mult))
            V(lambda e: e.tensor_copy(out=s5CA[:, 0, :], in_=LR[:, :, 16]))
            V(lambda e: e.tensor_copy(out=s5CA[:, 1, :], in_=LR[:, :, 16]))
            V(lambda e: e.tensor_scalar(out=s5CB[:, 0, :], in0=LI[:, :, 16], scalar1=-1.0, scalar2=None, op0=ALU.mult))
            P.task("vector", lambda e: e.tensor_copy(out=s5CB[:, 1, :], in_=LI[:, :, 16]), reads=[r_s], writes=[r_s, r_ca])
            xr, den, cr, ci, t1 = A([G, 1]), A([G, 1]), A([G, 1]), A([G, 1]), A([G, 1])
            lr1, li1 = LR[:, :, 9:10], LI[:, :, 9:10]
            V(lambda e: e.tensor_scalar(out=xr, in0=lr1, scalar1=-1.0, scalar2=None, op0=ALU.add))
            V(lambda e: e.tensor_tensor(out=den, in0=are, in1=are, op=ALU.mult))
            V(lambda e: e.tensor_tensor(out=t1, in0=aim, in1=aim, op=ALU.mult))
            V(lambda e: e.tensor_tensor(out=den, in0=den, in1=t1, op=ALU.add))
            V(lambda e: e.reciprocal(out=den, in_=den))
            V(lambda e: e.tensor_tensor(out=cr, in0=xr, in1=are, op=ALU.mult))
            V(lambda e: e.tensor_tensor(out=t1, in0=li1, in1=aim, op=ALU.mult))
            V(lambda e: e.tensor_tensor(out=cr, in0=cr, in1=t1, op=ALU.add))
            V(lambda e: e.tensor_tensor(out=cr, in0=cr, in1=den, op=ALU.mult))
            V(lambda e: e.tensor_tensor(out=ci, in0=li1, in1=are, op=ALU.mult))
            V(lambda e: e.tensor_tensor(out=t1, in0=xr, in1=aim, op=ALU.mult))
            V(lambda e: e.tensor_tensor(out=ci, in0=ci, in1=t1, op=ALU.subtract))
            V(lambda e: e.tensor_tensor(out=ci, in0=ci, in1=den, op=ALU.mult))
            Bbr, Bbi, T2 = A([G, 16]), A([G, 16]), A([G, 16])
            crb, cib = cr.broadcast_to([64, G, 16]), ci.broadcast_to([64, G, 16])
            V(lambda e: e.tensor_tensor(out=Bbr, in0=Bre, in1=crb, op=ALU.mult))
            V(lambda e: e.tensor_tensor(out=T2, in0=Bim, in1=cib, op=ALU.mult))
            V(lambda e: e.tensor_tensor(out=Bbr, in0=Bbr, in1=T2, op=ALU.subtract))
            V(lambda e: e.tensor_tensor(out=Bbi, in0=Bim, in1=crb, op=ALU.mult))
            V(lambda e: e.tensor_tensor(out=T2, in0=Bre, in1=cib, op=ALU.mult))
            V(lambda e: e.tensor_tensor(out=Bbi, in0=Bbi, in1=T2, op=ALU.add))
            Wa, Wb = A([GC, 8, 16]), A([GC, 8, 16])
            q2s = A([GC, 2, 128], BF16)
            bws = ar.alloc([GC, 2, 64], BF16)
            ms = ar.alloc([GC, 128], BF16)
            mtmp = ar.alloc([1, 128], F32)[:, 0, :]
            r_q2s, r_bws, r_ms, r_mt = res("q2s"), res("bws"), res("ms"), res("mtmp")
            def c_bw_v(g0):
                return c_bw[g0 // 32].rearrange("p (g r q) -> p g r q", r=2, q=64)[:, g0 % 32:g0 % 32 + GC]

            def c_m_v(g0):
                return c_m[g0 // 32].rearrange("p (g q) -> p g q", q=128)[:, g0 % 32:g0 % 32 + GC]

            def c_q2_v(g0):
                return c_q2[g0 // 16].rearrange("p (g r q) -> p g r q", r=2, q=128)[:, g0 % 16:g0 % 16 + GC]

            def cplx(outr, outi, Lr_, Li_, Xr_, Xi_, nouti=None):
                V(lambda e: e.tensor_tensor(out=outr, in0=Lr_, in1=Xr_, op=ALU.mult))
                V(lambda e: e.tensor_tensor(out=Wa, in0=Li_, in1=Xi_, op=ALU.mult))
                V(lambda e: e.tensor_tensor(out=outr, in0=outr, in1=Wa, op=ALU.subtract))
                V(lambda e: e.tensor_tensor(out=outi, in0=Lr_, in1=Xi_, op=ALU.mult))
                V(lambda e: e.tensor_tensor(out=Wb, in0=Li_, in1=Xr_, op=ALU.mult))
                V(lambda e: e.tensor_tensor(out=outi, in0=outi, in1=Wb, op=ALU.add))
                if nouti is not None:
                    V(lambda e: e.tensor_scalar(out=nouti, in0=outi, scalar1=-1.0, scalar2=None, op0=ALU.mult))

            for gc in range(G // GC):
                gs = slice(gc * GC, (gc + 1) * GC)
                bshape = [64, GC, 8, 16]
                LPr = LR[:, gs, 0:8].rearrange("p g (k o) -> p g k o", o=1).broadcast_to(bshape)
                LPi = LI[:, gs, 0:8].rearrange("p g (k o) -> p g k o", o=1).broadcast_to(bshape)
                LQr = LR[:, gs, 8:16].rearrange("p g (k o) -> p g k o", o=1).broadcast_to(bshape)
                LQi = LI[:, gs, 8:16].rearrange("p g (k o) -> p g k o", o=1).broadcast_to(bshape)
                LQ2r = LR[:, gs, 16:24].rearrange("p g (k o) -> p g k o", o=1).broadcast_to(bshape)
                LQ2i = LI[:, gs, 16:24].rearrange("p g (k o) -> p g k o", o=1).broadcast_to(bshape)
                Bbr_b = Bbr[:, gs, :].rearrange("p g (o c) -> p g o c", o=1).broadcast_to(bshape)
                Bbi_b = Bbi[:, gs, :].rearrange("p g (o c) -> p g o c", o=1).broadcast_to(bshape)
                Cre_b = Cre[:, gs, :].rearrange("p g (o c) -> p g o c", o=1).broadcast_to(bshape)
                Cim_b = Cim[:, gs, :].rearrange("p g (o c) -> p g o c", o=1).broadcast_to(bshape)
                cplx(Qr, Qi, LQ2r, LQ2i, Cre_b, Cim_b)
                P.task("vector", lambda e: e.tensor_copy(out=q2s[:, :, 0, :].rearrange("p g (j c) -> p g j c", c=16), in_=Qr),
                       reads=[r_s], writes=[r_q2s])
                P.task("vector", lambda e: e.tensor_scalar(out=q2s[:, :, 1, :].rearrange("p g (j c) -> p g j c", c=16), in0=Qi,
                                                           scalar1=-1.0, scalar2=None, op0=ALU.mult),
                       reads=[r_s], writes=[r_q2s])
                P.dma("sync", [(c_q2_v(gc * GC), q2s)], reads=[r_q2s], writes=[res("c_q2")], key="S_q2s", arena=True)
                cplx(Pr, Pi, LPr, LPi, Bbr_b, Bbi_b, nouti=nPi)
                cplx(Qr, Qi, LQr, LQi, Cre_b, Cim_b)
                for gl in range(GC):
                    g = gc * GC + gl
                    bank, r_b = next_ps()
                    P.task("tensor", [lambda e, gl=gl, bank=bank: e.transpose(out=bank[:, 0:64], in_=Pr[:, gl].rearrange("p i c -> p (i c)"),
                                                                              identity=ident32[0:64, 0:64]),
                                      lambda e, gl=gl, bank=bank: e.transpose(out=bank[:, 64:128], in_=Pi[:, gl].rearrange("p i c -> p (i c)"),
                                                                              identity=ident32[0:64, 0:64])],
                           reads=[r_s, r_const], writes=[r_b])
                    P.task("scalar", lambda e, gl=gl, bank=bank: e.copy(out=bws[:, gl].rearrange("p r q -> p (r q)"), in_=bank[:, 0:128]),
                           reads=[r_b], writes=[r_bws])
                    bank2, r_b2 = next_ps()
                    P.task("tensor", [lambda e, gl=gl, bank2=bank2: e.matmul(bank2[:, 0:128], Pr[:, gl].rearrange("p i c -> p (i c)"),
                                                                             Qr[:, gl].rearrange("p i c -> p (i c)"), start=True, stop=False),
                                      lambda e, gl=gl, bank2=bank2: e.matmul(bank2[:, 0:128], nPi[:, gl].rearrange("p i c -> p (i c)"),
                                                                             Qi[:, gl].rearrange("p i c -> p (i c)"), start=False, stop=True)],
                           reads=[r_s], writes=[r_b2])
                    P.task("vector", lambda e, bank2=bank2: e.tensor_tensor(out=mtmp, in0=bank2[:, 0:128], in1=cmask, op=ALU.mult),
                           reads=[r_b2, r_dm], writes=[r_mt])
                    P.task("vector", lambda e, gl=gl, g=g: e.scalar_tensor_tensor(out=ms[:, gl, :], in0=ident32[:, :], scalar=dcol[:, g:g + 1],
                                                                                  in1=mtmp, op0=ALU.mult, op1=ALU.add),
                           reads=[r_mt, r_dm, r_const], writes=[r_ms])
                P.dma("sync", [(c_bw_v(gc * GC), bws)], reads=[r_bws], writes=[res("c_bw")], key="S_bws", arena=True)
                P.dma("sync", [(c_m_v(gc * GC), ms)], reads=[r_ms], writes=[res("c_m")], key="S_ms", arena=True)

        def s5_enqueue():
            proj_blocks(c_s5_in, D)

            def enq(cc, nm, hlf, parts):
                def mk(slot, cc=cc, hlf=hlf, parts=parts):
                    return [(slot[0:parts, :], cc[hlf])]
                w_enqueue(mk, res(nm))
            enq(c_selin, "c_selin", 0, 128)
            enq(c_selin, "c_selin", 1, 128)
            enq(c_bw, "c_bw", 0, 128)
            enq(c_bw, "c_bw", 1, 128)
            for hm in range(2):
                enq(c_m, "c_m", hm, 128)
                enq(c_q2, "c_q2", 2 * hm, 64)
                enq(c_q2, "c_q2", 2 * hm + 1, 64)
            enq(c_selout, "c_selout", 0, 128)
            enq(c_selout, "c_selout", 1, 128)
            proj_blocks(c_s5_glu, D)
            proj_blocks(c_s5_out, D)

        def s5(lnidx, first):
            P.barrier()
            ar.reset()
            N = NCH
            u16 = ar.alloc([KT, TT], BF16)
            U16 = ar.alloc([64, N], BF16)
            hseq = hseq_ap
            H16 = ar.alloc([2, 64, N], BF16, parts=64)
            Y16 = ar.alloc([64, N], BF16)
            yg32 = ar.alloc([KT, TT], F32)
            yg16 = ar.alloc([KT, TT], BF16)
            z16 = ar.alloc([KT, TT], BF16)
            t1 = ar.alloc([2, 64], F32, parts=64)
            t2 = ar.alloc([2, 64], F32, parts=64)
            tmps = [(ar.alloc([TT], F32), ar.alloc([TT], F32)) for _ in range(2)]
            r_tmps = [(res("s_xb%d" % i), res("s_t2%d" % i)) for i in range(2)]
            r_u16, r_U16, r_H16, r_Y16 = res("u16"), res("U16"), res("H16"), res("Y16")
            sc = ln_scratch(t16=u16, tsq16=U16.rearrange("p g n -> p (g n)").rearrange("p (k t) -> p k t", t=TT), r_t16=r_u16, r_tsq16=r_U16)
            r_yg32, r_yg16, r_z16, r_t1, r_t2 = res("yg32"), res("yg16"), res("z16"), res("sc_t1"), res("sc_t2")
            def ev_u(ot, bank, r_b):
                P.task("scalar", lambda e, ot=ot, bank=bank: e.copy(out=u16[:, ot, :], in_=bank[:, :]), reads=[r_b], writes=[r_u16])
            proj(x16, r_x16, 2, ev_u)
            sl = [w_next(), w_next()]
            for tile in range(KT):
                bank, r_b = next_ps()
                ins = []
                for gq in range(8):
                    for i in range(8):
                        m = gq * 8 + i
                        slot = sl[m // 32][0]
                        sv = slot[:, :].rearrange("p (m q) -> p m q", q=128)
                        ins.append(lambda e, bank=bank, sv=sv, m=m, gq=gq, i=i, tile=tile: e.matmul(
                            bank[:, gq * N:(gq + 1) * N], sv[:, m % 32, :], u16[:, tile, i:TT:8], start=(i == 0), stop=(i == 7)))
                P.task("tensor", ins, reads=[sl[0][1], sl[1][1], r_u16], writes=[r_b])
                P.task("scalar", lambda e, bank=bank, tile=tile: e.copy(out=U16[:, tile * 8:(tile + 1) * 8, :].rearrange("p g n -> p (g n)"), in_=bank[:, :]),
                       reads=[r_b], writes=[r_U16])
            w_done(2)
            sl = [w_next(), w_next()]
            for ri in range(2):
                for go in range(8):
                    bank, r_b = next_ps()
                    ins = []
                    for gq in range(8):
                        g = go * 8 + gq
                        slot = sl[g // 32][0]
                        sv = slot[:, :].rearrange("p (g r q) -> p g r q", r=2, q=64)
                        ins.append(lambda e, bank=bank, sv=sv, g=g, gq=gq, ri=ri: e.matmul(
                            bank[0:64, gq * N:(gq + 1) * N], sv[:, g % 32, ri, :], U16[:, g, :], start=True, stop=True))
                    P.task("tensor", ins, reads=[sl[0][1], sl[1][1], r_U16], writes=[r_b])
                    P.task("scalar", lambda e, bank=bank, ri=ri, go=go: e.copy(
                        out=hseq[:, ri, go * 8:(go + 1) * 8, 1:N + 1], in_=bank[0:64, :].rearrange("p (g n) -> p g n", n=N)),
                        reads=[r_b], writes=[r_hs])
            w_done(2)
            for n in range(N):
                P.task("vector", lambda e, n=n: e.tensor_tensor(out=t1, in0=s5CA[:, :, :], in1=hseq[:, :, :, n], op=ALU.mult),
                       reads=[r_hs, r_ca], writes=[r_t1])
                P.task("vector", lambda e, n=n: e.tensor_tensor(out=t2, in0=s5CB[:, :, :], in1=hseq[:, ::-1, :, n], op=ALU.mult),
                       reads=[r_hs, r_ca], writes=[r_t2])
                P.task("vector", lambda e: e.tensor_tensor(out=t1, in0=t1, in1=t2, op=ALU.add), reads=[r_t1, r_t2], writes=[r_t1])
                P.task("vector", lambda e, n=n: e.tensor_tensor(out=hseq[:, :, :, n + 1], in0=t1, in1=hseq[:, :, :, n + 1], op=ALU.add),
                       reads=[r_t1, r_hs], writes=[r_hs])
            P.task("scalar", lambda e: e.copy(out=H16, in_=hseq[:, :, :, 0:N]), reads=[r_hs], writes=[r_H16])
            P.task("gpsimd", lambda e: e.tensor_copy(out=hseq[:, :, :, 0], in_=hseq[:, :, :, N]), reads=[r_hs, r_H16], writes=[r_hs])
            for hm in range(2):
                slm_, r_slm = w_next()
                mv = slm_[:, :].rearrange("p (g q) -> p g q", q=128)
                for hq in range(2):
                    slq_, r_slq = w_next()
                    qv = slq_[0:64, :].rearrange("p (g r q) -> p g r q", r=2, q=128)
                    for tl in range(2):
                        tile = hm * 4 + hq * 2 + tl
                        bank, r_b = next_ps()
                        ins = []
                        for gq in range(8):
                            g = tile * 8 + gq
                            o = bank[:, gq * N:(gq + 1) * N]
                            ins.append(lambda e, o=o, mv=mv, g=g: e.matmul(o, mv[:, g % 32, :], U16[:, g, :], start=True, stop=False))
                            ins.append(lambda e, o=o, qv=qv, g=g: e.matmul(o, qv[:, g % 16, 0, :], H16[:, 0, g, :], start=False, stop=False))
                            ins.append(lambda e, o=o, qv=qv, g=g: e.matmul(o, qv[:, g % 16, 1, :], H16[:, 1, g, :], start=False, stop=True))
                        P.task("tensor", ins, reads=[r_slm, r_slq, r_U16, r_H16], writes=[r_b])
                        P.task("scalar", lambda e, bank=bank, tile=tile: e.copy(out=Y16[:, tile * 8:(tile + 1) * 8, :].rearrange("p g n -> p (g n)"), in_=bank[:, :]),
                               reads=[r_b], writes=[r_Y16])
                w_done(3)
            sl = [w_next(), w_next()]
            for tile in range(KT):
                bank, r_b = next_ps()
                ins = []
                for j in range(8):
                    for gq in range(8):
                        m = gq * 8 + j
                        sv = sl[m // 32][0][:, :].rearrange("p (m q) -> p m q", q=128)
                        ins.append(lambda e, bank=bank, sv=sv, m=m, gq=gq, j=j, tile=tile: e.matmul(
                            bank[:, j:TT:8], sv[:, m % 32, :], Y16[:, tile * 8 + gq, :], start=(gq == 0), stop=(gq == 7),
                            skip_group_check=True))
                P.task("tensor", ins, reads=[sl[0][1], sl[1][1], r_Y16], writes=[r_b])
                gelu_tanh(bank[:, :], r_b, yg32[:, tile, :], r_yg32, yg16[:, tile, :], r_yg16, None, tmps[tile % 2], r_tmps[tile % 2])
            w_done(2)
            if "dbg_y" in sw:
                t_ = P.dma("sync", [(out_d[0:512, :].rearrange("(p a) d -> p (a d)", p=128), yg32.rearrange("p k t -> p (k t)"))],
                           reads=[r_yg32], key="S_dbg", arena=True)
                out_tickets.append(t_)
            def ev_glu(ot, bank, r_b):
                tm, r_tm = tmps[ot % 2][0], r_tmps[ot % 2][0]
                P.task("scalar", lambda e, ot=ot, bank=bank, tm=tm: e.activation(out=tm, in_=bank[:, :], func=AF.Sigmoid,
                                                                                 bias=bglu[:, ot:ot + 1], scale=1.0),
                       reads=[r_b, r_const], writes=[r_tm])
                P.task("vector", lambda e, ot=ot, tm=tm: e.tensor_tensor(out=z16[:, ot, :], in0=yg32[:, ot, :], in1=tm, op=ALU.mult),
                       reads=[r_tm, r_yg32], writes=[r_z16])
            proj(yg16, r_yg16, 2, ev_glu)
            proj(z16, r_z16, 2, residual_evac(C_MIX))
            layer_norm(h32, r_h32, lng[:, lnidx * 8:(lnidx + 1) * 8], lnb[:, lnidx * 8:(lnidx + 1) * 8], 0,
                       h32, r_h32, x16, r_x16, sc)

        def load_tile_dma(k):
            P.barrier(sync=True)
            pairs = []
            for kt in range(KT):
                for s_ in range(4):
                    r0 = k * TT + s_ * 128
                    pairs.append((h32[:, kt, s_ * 128:(s_ + 1) * 128],
                                  x_d[r0:r0 + 128, kt * 128:(kt + 1) * 128].rearrange("t p -> p t")))
            for pr in pairs:
                P.dma("sync", [pr], writes=[r_h32], key="L_h32", nc_ok=True)
            P.task("vector", lambda e: e.tensor_copy(out=x16[:, :, :], in_=h32[:, :, :]), reads=[r_h32], writes=[r_x16])

        def store_tile_dma(k):
            P.barrier(sync=True)
            pairs = []
            for kt in range(KT):
                for s_ in range(4):
                    r0 = k * TT + s_ * 128
                    pairs.append((out_d[r0:r0 + 128, kt * 128:(kt + 1) * 128].rearrange("t p -> p t"),
                                  h32[:, kt, s_ * 128:(s_ + 1) * 128]))
            t_ = None
            for pr in pairs:
                t_ = P.dma("sync", [pr], reads=[r_h32], key="S_h32", nc_ok=True)
            out_tickets.append(t_)

        def load_tile(k):
            P.barrier(sync=True)
            ar.reset()
            xin = ar.alloc([4, D], F32)
            r_xin = res("xin")
            if "xdma1" in sw:
                P.dma("sync", [(xin, x_d[k * TT:(k + 1) * TT, :].rearrange("(s p) d -> p s d", p=128))],
                      writes=[r_xin], key="L_xin", arena=True)
            else:
                P.dma("sync", [(xin[:, s_, :], x_d[k * TT + s_ * 128:k * TT + (s_ + 1) * 128, :]) for s_ in range(4)],
                      writes=[r_xin], key="L_xin", arena=True)
            for kt in range(0 if "noxpose" in sw else KT):
                bank, r_b = next_ps()
                P.task("tensor", [lambda e, s=s, kt=kt, bank=bank: e.transpose(
                    out=bank[:, s * 128:(s + 1) * 128], in_=xin[:, s, kt * 128:(kt + 1) * 128], identity=ident32[:, :])
                    for s in range(4)], reads=[r_xin, r_const], writes=[r_b])
                if "noact" not in sw:
                    P.task("scalar", lambda e, kt=kt, bank=bank: e.copy(out=h32[:, kt, :], in_=bank[:, :]), reads=[r_b], writes=[r_h32])
                if "nodve" not in sw:
                    P.task("vector", lambda e, kt=kt, bank=bank: e.tensor_copy(out=x16[:, kt, :], in_=bank[:, :]), reads=[r_b], writes=[r_x16])

        out_tickets = []

        def store_tile(k):
            P.barrier(sync=True)
            ar.reset()
            yout = ar.alloc([4, D], F32)
            r_yout = res("yout")
            for s in range(4):
                for hlf in range(2):
                    bank, r_b = next_ps()
                    P.task("tensor", [lambda e, s=s, q=q, hlf=hlf, bank=bank: e.transpose(
                        out=bank[:, q * 128:(q + 1) * 128], in_=h32[:, hlf * 4 + q, s * 128:(s + 1) * 128], identity=ident32[:, :])
                        for q in range(4)], reads=[r_h32, r_const], writes=[r_b])
                    P.task("scalar" if hlf else "vector",
                           (lambda e, s=s, hlf=hlf, bank=bank: e.copy(out=yout[:, s, hlf * 512:(hlf + 1) * 512], in_=bank[:, :])) if hlf else
                           (lambda e, s=s, hlf=hlf, bank=bank: e.tensor_copy(out=yout[:, s, hlf * 512:(hlf + 1) * 512], in_=bank[:, :])),
                           reads=[r_b], writes=[r_yout])
            if "xdma1" in sw:
                t = P.dma("sync", [(out_d[k * TT:(k + 1) * TT, :].rearrange("(s p) d -> p s d", p=128), yout)],
                          reads=[r_yout], key="S_yout", arena=True)
            else:
                t = P.dma("sync", [(out_d[k * TT + s_ * 128:k * TT + (s_ + 1) * 128, :], yout[:, s_, :]) for s_ in range(4)],
                          reads=[r_yout], key="S_yout", arena=True)
            out_tickets.append(t)

        hseq_t = sb("hseq", [64, 2 * 64 * (NCH + 1)])
        hseq_ap = hseq_t[:, :].rearrange("p (r g n) -> p r g n", r=2, g=64)
        if "mix0" in stages:
            P.task("vector", lambda e: e.memset(hseq_t[:, :], 0.0), reads=[], writes=[r_hs])
            s5_setup()
        if "mix1" in stages:
            gmlp_setup()
        for k in range(nt):
            for st in stages:
                if st[0] == "f":
                    ffn_enqueue(int(st[1]), int(st[2]))
                elif st == "mix0":
                    s5_enqueue()
                else:
                    gmlp_enqueue()
        for k in range(nt):
            if "skipload" not in sw:
                (load_tile if "peio" in sw else load_tile_dma)(k)
            for st in stages:
                if st[0] == "f":
                    l, f = int(st[1]), int(st[2])
                    ffn(l, f, l * 3 + (0 if f == 0 else 2))
                elif st == "mix0":
                    s5(1, k == 0)
                else:
                    gmlp(4)
            if "skipstore" not in sw:
                (store_tile if "peio" in sw else store_tile_dma)(k)
        if dbg_d:
            pass
        P.wait_all("sync", out_tickets)
        P.emit()
    nc._in_names = in_names
    return nc


def _consts():
    ident = np.eye(128, dtype=np.float32)
    kvec = np.tile(np.asarray(KLIST, np.float32)[None, :], (64, 1))
    ii = np.arange(128) // 16
    cmask = (ii[None, :] >= ii[:, None]).astype(np.float32)
    selin = np.zeros((128, 64, 128), np.float32)
    for gq in range(8):
        for i in range(8):
            for c in range(16):
                selin[gq * 16 + c, gq * 8 + i, i * 16 + c] = 1.0
    selout = np.ascontiguousarray(np.transpose(selin, (2, 1, 0)))
    return dict(ident=ident, kvec=kvec, cmask=cmask, selin=selin.reshape(128, -1), selout=selout.reshape(128, -1))


def _prep_inputs(inp):
    f = lambda a: np.ascontiguousarray(np.asarray(a, dtype=np.float32))
    col = lambda v, n: f(np.asarray(v).reshape(n, 128).T)
    shared = dict(
        ffn_w_in=f(inp["ffn_w_in"]), ffn_w_out=f(inp["ffn_w_out"]),
        s5_w_in=f(inp["s5_w_in"][0]), s5_w_glu=f(inp["s5_w_glu"][0]), s5_w_out=f(inp["s5_w_out"][0]),
        g_w_in=f(inp["g_w_in"][0]), g_w_out=f(inp["g_w_out"][0]),
        smallc=f(np.concatenate([col(inp["ln_g"], 48), col(inp["ln_b"], 48), col(inp["s5_b_glu"][0], 8),
                                 col(inp["g_b_in"][0], 16), col(inp["g_ln_g"][0], 8), col(inp["g_ln_b"][0], 8)], axis=1)),
        gws=f(np.transpose(np.asarray(inp["g_w_s"][0]), (1, 0, 2))),
        gbs=f(np.broadcast_to(np.asarray(inp["g_b_s"][0])[None], (128, 8, 128))),
        s5are=f(np.asarray(inp["s5_a_re"][0]).T), s5aim=f(np.asarray(inp["s5_a_im"][0]).T),
        s5ldt=f(np.broadcast_to(np.asarray(inp["s5_log_dt"][0])[None, :], (64, 64))),
        s5bre=f(np.transpose(np.asarray(inp["s5_b_re"][0]), (1, 0, 2)).reshape(64, -1)),
        s5bim=f(np.transpose(np.asarray(inp["s5_b_im"][0]), (1, 0, 2)).reshape(64, -1)),
        s5cre=f(np.transpose(np.asarray(inp["s5_c_re"][0]), (2, 0, 1)).reshape(64, -1)),
        s5cim=f(np.transpose(np.asarray(inp["s5_c_im"][0]), (2, 0, 1)).reshape(64, -1)),
        s5dcol=f(np.tile(np.asarray(inp["s5_d"][0]).reshape(64, 16).T, (8, 1))),
    )
    shared.update(_consts())
    return shared


_NC_CACHE = {}


def kernel(**inputs):
    shared = _prep_inputs(inputs)
    x = np.asarray(inputs["x"], dtype=np.float32)
    if "nc" not in _NC_CACHE:
        _NC_CACHE["nc"] = build()
    nc = _NC_CACHE["nc"]
    in_maps = []
    for c in range(8):
        m = {k: v for k, v in shared.items() if k in nc._in_names}
        m["x_tok"] = np.ascontiguousarray(x[c])
        in_maps.append(m)
    res = run_bass_kernel_spmd(nc, in_maps, core_ids=list(range(8)))
    out = np.stack([np.asarray(r["y_tok"]) for r in res.results], axis=0)
    return out.astype(np.float32)
```
